# Optimizing a Trainium2 kernel written in Bass

```python
import jax, jax.numpy as jnp
from jax import lax
import numpy as np

D_MODEL = 1024
BATCH = 8
SEQ = 4096
DEPTH = 1

CHUNK = 64
RET_WIDTH = D_MODEL // 2
RET_HEAD_DIM = 128
RET_HEADS = RET_WIDTH // RET_HEAD_DIM
GMLP_WIDTH = D_MODEL - RET_WIDTH
GMLP_GROUP_DIM = 128
GMLP_GROUPS = GMLP_WIDTH // GMLP_GROUP_DIM
GMLP_BLOCK = 128
MIX_WIDTH = RET_WIDTH + GMLP_WIDTH
PROJ_WIDTH = 4 * RET_WIDTH + 2 * GMLP_WIDTH
D_FF = 256 * ((8 * D_MODEL + 3 * 256 - 1) // (3 * 256))
ROPE_THETA = 10000.0
EPS = 1e-6

kernel_name = "hybrid_retention_gmlp_block"


def rmsnorm(x, g):
    xf = x.astype(jnp.float32)
    y = xf * lax.rsqrt(jnp.mean(xf * xf, axis=-1, keepdims=True) + EPS)
    return (y * g.astype(jnp.float32)).astype(x.dtype)


def layernorm(x, g, b):
    xf = x.astype(jnp.float32)
    mu = jnp.mean(xf, axis=-1, keepdims=True)
    var = jnp.mean(jnp.square(xf - mu), axis=-1, keepdims=True)
    y = (xf - mu) * lax.rsqrt(var + EPS)
    return (y * g.astype(jnp.float32) + b.astype(jnp.float32)).astype(x.dtype)


def head_groupnorm(y, g):
    B, S, H, D = y.shape
    yf = y.astype(jnp.float32)
    mu = jnp.mean(yf, axis=-1, keepdims=True)
    var = jnp.mean(jnp.square(yf - mu), axis=-1, keepdims=True)
    yn = ((yf - mu) * lax.rsqrt(var + EPS)).reshape(B, S, H * D)
    return (yn * g.astype(jnp.float32)).astype(y.dtype)


def rope(x, pos):
    D = x.shape[-1]
    inv = ROPE_THETA ** (-jnp.arange(0, D, 2, dtype=jnp.float32) / D)
    ang = pos.astype(jnp.float32)[:, None] * inv[None, :]
    cos = jnp.cos(ang).astype(x.dtype)[None, :, None, :]
    sin = jnp.sin(ang).astype(x.dtype)[None, :, None, :]
    x1, x2 = x[..., : D // 2], x[..., D // 2:]
    return jnp.concatenate([x1 * cos - x2 * sin, x2 * cos + x1 * sin], axis=-1)


def retention(q, k, v):
    B, S, H, D = q.shape
    N = S // CHUNK
    dt = q.dtype
    log_gamma = jnp.log(1.0 - jnp.power(2.0, -5.0 - jnp.arange(H, dtype=jnp.float32)))
    idx = jnp.arange(CHUNK, dtype=jnp.float32)
    inner_decay = jnp.exp(log_gamma[:, None, None] * jnp.abs(idx[:, None] - idx[None, :])).astype(dt)
    k_decay = jnp.exp(log_gamma[:, None] * (CHUNK - 1 - idx)[None, :]).astype(dt)
    q_decay = jnp.exp(log_gamma[:, None] * (idx + 1)[None, :]).astype(dt)
    chunk_decay = jnp.exp(log_gamma * CHUNK).astype(dt)

    qc = (q * (D ** -0.5)).reshape(B, N, CHUNK, H, D)
    kc = k.reshape(B, N, CHUNK, H, D)
    vc = v.reshape(B, N, CHUNK, H, D)

    scores = jnp.einsum('bnihd,bnjhd->bnhij', qc, kc) * inner_decay[None, None]
    inner = jnp.einsum('bnhij,bnjhe->bnihe', scores, vc)

    kv = jnp.einsum('bnjhd,bnjhe,hj->nbhde', kc, vc, k_decay)

    def step(state, u):
        return chunk_decay[:, None, None] * state + u, state

    _, prev = lax.scan(step, jnp.zeros(kv.shape[1:], kv.dtype), kv)
    cross = jnp.einsum('bnihd,nbhde,hi->bnihe', qc, prev, q_decay)
    return (inner + cross).reshape(B, S, H, D)


def spatial_gating(u, v, w_s, b_s, ln_g, ln_b):
    B, S, _ = v.shape
    M = S // GMLP_BLOCK
    v = layernorm(v, ln_g, ln_b)
    vb = v.reshape(B, M, GMLP_BLOCK, GMLP_GROUPS, GMLP_GROUP_DIM)
    cpos = jnp.arange(GMLP_BLOCK) // CHUNK
    mask = cpos[:, None] >= cpos[None, :]
    w = jnp.where(mask[None], w_s, jnp.zeros((), w_s.dtype))
    mixed = jnp.einsum('gts,bmsgc->bmtgc', w, vb) + b_s.T[None, None, :, :, None]
    return u * mixed.reshape(B, S, GMLP_WIDTH)


def setup_inputs(seed: int = 0) -> dict:
    key = jax.random.key(seed)
    ks = jax.random.split(key, 16)
    f32 = jnp.float32
    nrm = lambda k, shape, s: jax.random.normal(k, shape, f32) * s
    return {
        "x": jax.random.normal(ks[0], (BATCH, SEQ, D_MODEL), f32),
        "norm1_g": 1.0 + nrm(ks[1], (DEPTH, D_MODEL), 0.05),
        "w_in": nrm(ks[2], (DEPTH, D_MODEL, PROJ_WIDTH), D_MODEL ** -0.5),
        "ret_gn_g": 1.0 + nrm(ks[3], (DEPTH, RET_WIDTH), 0.05),
        "gmlp_ln_g": 1.0 + nrm(ks[4], (DEPTH, GMLP_WIDTH), 0.05),
        "gmlp_ln_b": nrm(ks[5], (DEPTH, GMLP_WIDTH), 0.02),
        "w_s": nrm(ks[6], (DEPTH, GMLP_GROUPS, GMLP_BLOCK, GMLP_BLOCK), GMLP_BLOCK ** -0.5),
        "b_s": 1.0 + nrm(ks[7], (DEPTH, GMLP_GROUPS, GMLP_BLOCK), 0.1),
        "w_out": nrm(ks[8], (DEPTH, MIX_WIDTH, D_MODEL), MIX_WIDTH ** -0.5),
        "norm2_g": 1.0 + nrm(ks[9], (DEPTH, D_MODEL), 0.05),
        "w_ffn_gate": nrm(ks[10], (DEPTH, D_MODEL, D_FF), D_MODEL ** -0.5),
        "w_ffn_up": nrm(ks[11], (DEPTH, D_MODEL, D_FF), D_MODEL ** -0.5),
        "w_ffn_down": nrm(ks[12], (DEPTH, D_FF, D_MODEL), D_FF ** -0.5),
        "final_g": 1.0 + nrm(ks[13], (D_MODEL,), 0.05),
    }


def reference(x, norm1_g, w_in, ret_gn_g, gmlp_ln_g, gmlp_ln_b, w_s, b_s, w_out,
              norm2_g, w_ffn_gate, w_ffn_up, w_ffn_down, final_g):
    B, S, _ = x.shape
    pos = jnp.arange(S)
    splits = [RET_WIDTH, 2 * RET_WIDTH, 3 * RET_WIDTH, 4 * RET_WIDTH, 4 * RET_WIDTH + GMLP_WIDTH]
    for l in range(DEPTH):
        h = rmsnorm(x, norm1_g[l])
        p = h @ w_in[l]
        q, k, v, g, u, vg = jnp.split(p, splits, axis=-1)
        q = rope(q.reshape(B, S, RET_HEADS, RET_HEAD_DIM), pos)
        k = rope(k.reshape(B, S, RET_HEADS, RET_HEAD_DIM), pos)
        v = v.reshape(B, S, RET_HEADS, RET_HEAD_DIM)
        ret = head_groupnorm(retention(q, k, v), ret_gn_g[l]) * jax.nn.silu(g)
        gm = spatial_gating(jax.nn.gelu(u), jax.nn.gelu(vg), w_s[l], b_s[l],
                            gmlp_ln_g[l], gmlp_ln_b[l])
        x = x + jnp.concatenate([ret, gm], axis=-1) @ w_out[l]
        h2 = rmsnorm(x, norm2_g[l])
        x = x + (jax.nn.silu(h2 @ w_ffn_gate[l]) * (h2 @ w_ffn_up[l])) @ w_ffn_down[l]
    return rmsnorm(x, final_g)
```

```python
import numpy as np
from contextlib import ExitStack
import concourse.bass as bass
import concourse.mybir as mybir
from concourse.bass_utils import run_bass_kernel_spmd

F32 = mybir.dt.float32
BF16 = mybir.dt.bfloat16
AF = mybir.ActivationFunctionType
ALU = mybir.AluOpType

D = 1024
SEQ = 4096
T = 512
NSUB = 4
NT_FULL = SEQ // T
DFF = 2816
NFC = DFF // 128
HALF = NFC // 2
EPS = 1e-6
NRING = 4


class Sched:
    ENGS = ("sync", "tensor", "vector", "scalar", "gpsimd")

    def __init__(self, nc, es):
        self.nc = nc
        self.es = es
        self.ops = {e: [] for e in self.ENGS}
        self.sems = {}
        self.count = {}
        self.known = {e: {} for e in self.ENGS}
        self.res = {}
        self.snap = {}
        for e in self.ENGS:
            self._mk(e)

    def _mk(self, name):
        if name not in self.sems:
            self.sems[name] = self.es.enter_context(self.nc.semaphore("s_" + name))
            self.count[name] = 0

    def _r(self, name):
        if name not in self.res:
            self.res[name] = {"w": None, "r": []}
        return self.res[name]

    def op(self, eng, fn, reads=(), writes=(), dma=None, ndma=1):
        need = {}

        def want(idv, raw):
            if idv is None:
                return
            ch, val = idv
            if ch == eng and not raw and eng == "tensor":
                return
            if self.known[eng].get(ch, 0) >= val:
                return
            need[ch] = max(need.get(ch, 0), val)

        for r in reads:
            want(self._r(r)["w"], True)
        for w in writes:
            rr = self._r(w)
            want(rr["w"], False)
            for rd in rr["r"]:
                want(rd, False)
        if dma is not None:
            self._mk(dma)
            self.count[dma] += 16 * ndma
            my = (dma, self.count[dma])
            inc = (dma, 16)
        else:
            self.count[eng] += 1
            my = (eng, self.count[eng])
            inc = (eng, 1)
        implied = {}
        for ch, val in need.items():
            for c2, v2 in self.snap.get((ch, val), {}).items():
                if v2 > implied.get(c2, 0):
                    implied[c2] = v2
        need = {ch: val for ch, val in need.items() if implied.get(ch, 0) < val}
        kn = self.known[eng]
        for ch, val in list(need.items()) + list(implied.items()):
            if val > kn.get(ch, 0):
                kn[ch] = val
        self.snap[my] = dict(kn)
        for r in reads:
            self._r(r)["r"].append(my)
        for w in writes:
            rr = self._r(w)
            rr["w"] = my
            rr["r"] = []
        self.ops[eng].append((list(need.items()), fn, inc, dma is not None, my))
        return my

    def final_wait(self, eng, chans):
        need = [(ch, self.count[ch]) for ch in chans if self.count.get(ch, 0) > 0]
        self.ops[eng].append((need, None, None, False, None))

    def emit(self):
        nc = self.nc

        HOIST = 0

        def hoist_plan(name):
            ops = self.ops[name]
            early = {j: [] for j in range(len(ops))}
            kept = {}
            for j, (waits, fn, inc, is_dma, my) in enumerate(ops):
                kept[j] = list(waits)
                if fn is None or name != "tensor":
                    continue
                keep = []
                for (ch, val) in waits:
                    dep = self.snap.get((ch, val), {}).get(name, 0)
                    p = j
                    while p > 0 and j - p < HOIST:
                        prev = ops[p - 1]
                        if prev[1] is None or prev[4] is None or prev[4][0] != name:
                            break
                        if dep >= prev[4][1]:
                            break
                        p -= 1
                    if p < j:
                        early[p].append((ch, val))
                    else:
                        keep.append((ch, val))
                kept[j] = keep
            return early, kept

        def replay(name, e):
            early, kept = hoist_plan(name)
            for j, (waits, fn, inc, is_dma, my) in enumerate(self.ops[name]):
                for ch, val in early[j]:
                    e.wait_ge(self.sems[ch], val)
                waits = kept[j]
                if fn is None:
                    for ch, val in waits:
                        e.wait_ge(self.sems[ch], val)
                    continue
                for ch, val in waits[1:]:
                    e.wait_ge(self.sems[ch], val)
                r = fn(e)
                ins = r if isinstance(r, (list, tuple)) else [r]
                if waits:
                    ch, val = waits[0]
                    ins[0]._wait_ge(self.sems[ch], val)
                if is_dma:
                    for i_ in ins:
                        i_.then_inc(self.sems[inc[0]], 16)
                else:
                    ins[-1].then_inc(self.sems[inc[0]], 1)

        with nc.Block() as block:
            @block.sync
            def _(e):
                replay("sync", e)

            @block.tensor
            def _(e):
                replay("tensor", e)

            @block.vector
            def _(e):
                replay("vector", e)

            @block.scalar
            def _(e):
                replay("scalar", e)

            @block.gpsimd
            def _(e):
                replay("gpsimd", e)


def _consts():
    inv = (10000.0 ** (-np.arange(0, 128, 2, dtype=np.float32) / np.float32(128))).astype(np.float32)
    pos = np.arange(SEQ, dtype=np.float32)
    ang = (pos[:, None] * inv[None, :]).astype(np.float32)
    cos = np.cos(ang).astype(np.float32).T
    sin = np.sin(ang).astype(np.float32).T
    ct = np.concatenate([cos, cos], axis=0)
    st = np.concatenate([sin, -sin], axis=0)
    ct = np.ascontiguousarray(ct.reshape(128, NT_FULL, T).transpose(1, 0, 2))
    st = np.ascontiguousarray(st.reshape(128, NT_FULL, T).transpose(1, 0, 2))
    gam = 1.0 - 2.0 ** (-5.0 - np.arange(4, dtype=np.float64))
    i = np.arange(128)
    ci = i // 64
    maskT = np.zeros((128, 4, 128), np.float64)
    for h in range(4):
        dif = i[None, :] - i[:, None]
        same = ci[None, :] == ci[:, None]
        later = ci[None, :] > ci[:, None]
        m = np.where(same, gam[h] ** np.abs(dif), np.where(later, gam[h] ** np.maximum(dif, 0), 0.0))
        maskT[:, h, :] = m * (128.0 ** -0.5)
    qdec = np.zeros((128, 4, 128), np.float64)
    kdec = np.zeros((128, 4), np.float64)
    for h in range(4):
        qdec[:, h, :] = (gam[h] ** (i + 1.0))[None, :] * (128.0 ** -0.5)
        kdec[:, h] = gam[h] ** (127.0 - i)
    cd = [float(gam[h] ** 128.0) for h in range(4)]
    return dict(ct=ct, st=st, maskT=maskT.astype(np.float32), qdec=qdec.astype(np.float32),
                kdec=kdec.astype(np.float32), cd=cd)


_CD = [float((1.0 - 2.0 ** (-5.0 - h)) ** 128.0) for h in range(4)]


def build_program(NT=NT_FULL):
    nc = bass.Bass("TRN2", target_bir_lowering=False)

    def din(name, shape):
        return nc.dram_tensor(name, list(shape), F32, kind="ExternalInput").ap()

    x_d = din("x", [NT * T, D])
    win_d = din("w_in_l", [128, 8, 3072])
    wout_d = din("w_out_l", [128, 8, 1024])
    wgu_d = din("w_gu_l", [128, NFC, 2048])
    wdn_d = din("w_dn_l", [128, NFC, 1024])
    ct_d = din("ct", [NT_FULL, 128, T])
    st_d = din("st", [NT_FULL, 128, T])
    maskT_d = din("maskT", [128, 4, 128])
    qdec_d = din("qdec", [128, 4, 128])
    bbc_d = din("bbc", [128, 4, 128])
    wsT_d = din("wsT", [128, 4, 128])
    fgb_d = din("fgb", [128, 1024])
    cols_d = din("cols", [128, 32])
    ident_d = din("ident", [128, 128])
    y_d = nc.dram_tensor("y", [NT * T, D], F32, kind="ExternalOutput").ap()
    wout_s = nc.dram_tensor("wout_s", [128, 8, 1024], BF16, kind="Internal").ap()
    wgu_s = nc.dram_tensor("wgu_s", [128, NFC, 2048], BF16, kind="Internal").ap()
    wdn_s = nc.dram_tensor("wdn_s", [128, NFC, 1024], BF16, kind="Internal").ap()

    x_t = x_d.rearrange("(i s p) d -> i s p d", s=NSUB, p=128)
    y_t = y_d.rearrange("(i s p) d -> i s p d", s=NSUB, p=128)

    es = ExitStack()
    with es:
        def sb(name, shape, dt=F32):
            return es.enter_context(nc.sbuf_tensor(name, list(shape), dt))

        WIN = sb("WIN", [128, 8, 3072], BF16)
        RING = [sb(f"RING{j}", [128, 4096], BF16) for j in range(NRING)]
        XB = sb("XB", [128, NSUB, 1024])
        XSTG = [sb(f"XSTG{j}", [128, 1024]) for j in range(2)]
        HT = sb("HT", [128, 8, T], BF16)
        H2T = sb("H2T", [128, 8, T], BF16)
        XS = [sb(f"XS{j}", [128, 1024], BF16) for j in range(2)]
        MIXT = sb("MIXT", [128, 8, T], BF16)
        HID = sb("HID", [128, HALF, T], BF16)
        SG = [sb(f"SG{j}", [128, T]) for j in range(2)]
        CT = sb("CT", [128, T])
        ST = sb("ST", [128, T])
        T1 = [sb(f"T1_{j}", [128, T]) for j in range(2)]
        T2 = [sb(f"T2_{j}", [128, T]) for j in range(2)]
        QT = [sb(f"QT{j}", [128, T], BF16) for j in range(2)]
        KT = [sb(f"KT{j}", [128, T], BF16) for j in range(2)]
        QDT = [sb(f"QDT{j}", [128, T], BF16) for j in range(2)]
        KD = [sb(f"KD{j}", [128, NSUB, 128], BF16) for j in range(2)]
        V = sb("V", [128, NSUB, 512], BF16)
        SGT = [sb(f"SGT{j}", [128, T], BF16) for j in range(2)]
        GUT = [sb(f"GUT{j}", [128, T], BF16) for j in range(2)]
        NN = sb("NN", [128, NSUB, 512], BF16)
        GV = [sb(f"GV{j}", [128, 512]) for j in range(2)]
        STM = [sb(f"STM{j}", [128, NSUB, 128], BF16) for j in range(2)]
        RN = [sb(f"RN{j}", [128, NSUB, 128], BF16) for j in range(2)]
        PST = sb("PST", [128, 4, 2, 128])
        PBH = sb("PBH", [128, 4, NSUB, 128], BF16)
        FGB = sb("FGB", [128, 1024])
        MASKT = sb("MASKT", [128, 4, 128])
        QDEC = sb("QDEC", [128, 4, 128])
        B2 = sb("B2", [128, 4, 128])
        WMT = sb("WMT", [128, 4, 128], BF16)
        IDENT = sb("IDENT", [128, 128], BF16)
        ONES = sb("ONES", [128, 128], BF16)
        COLS = sb("COLS", [128, 32])
        MHALF = sb("MHALF", [128, 8])
        STAT = sb("STAT", [128, 64])
        BNS = sb("BNS", [128, 8, 6])
        MV = sb("MV", [128, 8, 2])

        PS = [es.enter_context(nc.psum_tensor(f"PS{j}", [128, 512], F32)) for j in range(8)]

        def psb(j):
            return PS[j][:, :].bitcast(BF16)

        S = Sched(nc, es)
        G1C, G2C, GNC, LGC, LBC, KDC = 0, 8, 16, 20, 24, 28

        def pow_rstd(out_ap, in_ap, n, reads, writes, scale=None):
            sc = 1.0 if scale is None else scale
            S.op("vector", lambda e: e.tensor_scalar(out=out_ap, in0=in_ap, scalar1=sc, scalar2=EPS,
                                                     op0=ALU.mult, op1=ALU.add), reads=reads, writes=writes)
            S.op("gpsimd", lambda e: e.tensor_tensor(out=out_ap, in0=out_ap, in1=MHALF[:, 0:n], op=ALU.pow),
                 reads=writes + ["MHALF"], writes=writes)

        def norm_coefs(rs_ap, var_ap, nm_ap, mean_ap, n, reads, rname, nname):
            S.op("vector", lambda e: [e.tensor_scalar(out=rs_ap, in0=var_ap, scalar1=1.0, scalar2=EPS, op0=ALU.mult, op1=ALU.add),
                                      e.tensor_scalar(out=nm_ap, in0=mean_ap, scalar1=-1.0, scalar2=None, op0=ALU.mult)],
                 reads=reads, writes=[rname, nname])
            S.op("gpsimd", lambda e: e.tensor_tensor(out=rs_ap, in0=rs_ap, in1=MHALF[:, 0:n], op=ALU.pow),
                 reads=[rname, "MHALF"], writes=[rname])
            S.op("gpsimd", lambda e: e.tensor_tensor(out=nm_ap, in0=nm_ap, in1=rs_ap, op=ALU.mult),
                 reads=[rname, nname], writes=[nname])

        def prepA(i, s, from_xb=False):
            xs = XS[s % 2]
            xn = f"XS{s % 2}"
            if from_xb:
                stg_ap = XB[:, s, :]
                sn = f"XB{s}"
            else:
                stg_ap = XSTG[s % 2][:, :]
                sn = f"XSTG{s % 2}"
                S.op("sync", lambda e: e.dma_start(out=stg_ap, in_=x_t[i, s]), writes=[sn], dma=f"x_{s % 2}")
            S.op("scalar", lambda e: e.activation(out=xs[:, :], in_=stg_ap, func=AF.Square,
                                                  accum_out=STAT[:, s:s + 1]),
                 reads=[sn], writes=[xn, f"ss1_{s}"])
            pow_rstd(STAT[:, 4 + s:5 + s], STAT[:, s:s + 1], 1, [f"ss1_{s}"], [f"rs1_{s}"], scale=1.0 / D)
            S.op("scalar", lambda e: e.activation(out=xs[:, :], in_=stg_ap, func=AF.Copy,
                                                  scale=STAT[:, 4 + s:5 + s]),
                 reads=[sn, f"rs1_{s}"], writes=[xn])

        def prepB(i, s):
            xs = XS[s % 2]
            xn = f"XS{s % 2}"
            S.op("tensor", lambda e: [e.transpose(out=psb(4)[:, k * 128:(k + 1) * 128], in_=xs[:, k * 128:(k + 1) * 128],
                                                  identity=IDENT[:, :]) for k in range(8)],
                 reads=[xn, "IDENT"], writes=["P4"])
            S.op("vector", lambda e: e.tensor_tensor(out=HT[:, :, s * 128:(s + 1) * 128],
                                                     in0=psb(4).rearrange("p (k t) -> p k t", k=8),
                                                     in1=COLS[:, G1C:G1C + 8].unsqueeze(2).to_broadcast([128, 8, 128]),
                                                     op=ALU.mult),
                 reads=["P4", "COLS"], writes=[f"HT{s}"])

        def ld(eng, dst, src, sem, reads=(), writes=()):
            S.op(eng, lambda e: e.dma_start(out=dst, in_=src), reads=reads, writes=writes, dma=sem)

        S.op("vector", lambda e: e.memset(MHALF[:, :], -0.5), writes=["MHALF"])
        ld("sync", COLS[:, :], cols_d, "c_cols", writes=["COLS"])
        for s in range(NSUB):
            S.op("gpsimd", lambda e, s=s: e.dma_start(out=XB[:, s, :], in_=x_t[0, s]), writes=[f"XB{s}"], dma=f"xb_{s}")
        ld("gpsimd", IDENT[:, :], ident_d, "c_id", writes=["IDENT"])
        prepA(0, 0, True)
        prepA(0, 1, True)
        for G in (2, 1, 0, 3, 5, 4):
            ld("gpsimd", WIN[:, :, G * 512:(G + 1) * 512], win_d[:, :, G * 512:(G + 1) * 512],
               f"c_win{G}", writes=[f"WIN{G}"])
        ld("sync", CT[:, :], ct_d[0], "t_c", writes=["CT"])
        ld("sync", ST[:, :], st_d[0], "t_s", writes=["ST"])
        ld("sync", MASKT[:, :, :], maskT_d, "c_mask", writes=["MASKT"])
        ld("sync", QDEC[:, :, :], qdec_d, "c_qdec", writes=["QDEC"])
        ld("sync", T1[0][:, :], bbc_d.rearrange("p g t -> p (g t)"), "c_bbc", writes=["T1_0"])
        ld("sync", T2[0][:, :], wsT_d.rearrange("p g t -> p (g t)"), "c_wst", writes=["T2_0"])
        ld("sync", FGB[:, :], fgb_d, "c_fgb", writes=["FGB"])
        stream = []
        for b in range(2):
            stream.append((f"wo{b}", wout_s[:, 4 * b:4 * b + 4, :], wout_d[:, 4 * b:4 * b + 4, :]))
        for hf in range(2):
            c0 = hf * HALF
            for b in range(0, HALF, 2):
                n = min(2, HALF - b)
                stream.append((f"gu{hf}_{b}", wgu_s[:, c0 + b:c0 + b + n, :], wgu_d[:, c0 + b:c0 + b + n, :]))
            for b in range(0, HALF, 4):
                n = min(4, HALF - b)
                stream.append((f"dn{hf}_{b}", wdn_s[:, c0 + b:c0 + b + n, :], wdn_d[:, c0 + b:c0 + b + n, :]))
        for (nm, dst, src) in stream:
            ld("gpsimd", dst, src, "cv_" + nm, writes=["scr_" + nm])

        S.op("vector", lambda e: e.memset(ONES[:, :], 1.0), writes=["ONES"])
        S.op("vector", lambda e: e.memset(PST[:, :, :, :].rearrange("p a b c -> p (a b c)"), 0.0), writes=[f"PST{h}_0" for h in range(4)] + [f"PST{h}_1" for h in range(4)])
        S.op("vector", lambda e: e.memset(PBH[:, :, :, :].rearrange("p a b c -> p (a b c)"), 0.0), writes=[f"PBH{h}_{s}" for h in range(4) for s in range(NSUB)])
        S.op("vector", lambda e: e.tensor_copy(out=WMT[:, :, :].rearrange("p g t -> p (g t)"), in_=T2[0][:, :]),
             reads=["T2_0"], writes=["WMT"])
        S.op("vector", lambda e: e.memset(WMT[64:128, :, 0:64], 0.0), reads=["WMT"], writes=["WMT"])
        S.op("tensor", lambda e: e.matmul(PS[0][:, :], lhsT=ONES[:, :], rhs=WMT[:, :, :].rearrange("p g t -> p (g t)"),
                                          start=True, stop=True), reads=["ONES", "WMT"], writes=["P0"])

        def b2f(e):
            out = []
            for g in range(4):
                out.append(e.scalar_tensor_tensor(out=B2[:, g, :], in0=PS[0][:, g * 128:(g + 1) * 128],
                                                  scalar=COLS[:, LBC + g:LBC + g + 1], in1=T1[0][:, g * 128:(g + 1) * 128],
                                                  op0=ALU.mult, op1=ALU.add))
            return out
        S.op("vector", b2f, reads=["P0", "COLS", "T1_0"], writes=["B2"])

        ring_ctr = [0]

        def ring_load(nm, ncols):
            j = ring_ctr[0] % NRING
            ring_ctr[0] += 1
            src = dict((a, b) for a, b, _ in stream)[nm]
            src2 = src.rearrange("p c n -> p (c n)")
            S.op("sync", lambda e: e.dma_start(out=RING[j][:, 0:ncols], in_=src2),
                 reads=["scr_" + nm], writes=[f"RING{j}"], dma=f"r_{j}")
            return j

        HTall = [f"HT{s}" for s in range(NSUB)]
        H2Tall = [f"H2T{s}" for s in range(NSUB)]
        pbank = [0]

        def big_bank():
            b = pbank[0] % 3
            pbank[0] += 1
            return b

        def win_fm(col0, G):
            b = big_bank()
            S.op("tensor", lambda e: [e.matmul(PS[b][:, :], lhsT=WIN[:, k, col0:col0 + 128], rhs=HT[:, k, :],
                                               start=(k == 0), stop=(k == 7)) for k in range(8)],
                 reads=HTall + [f"WIN{G}"], writes=[f"P{b}"])
            return b

        def win_tm(col0, G, s):
            b = big_bank()
            S.op("tensor", lambda e: [e.matmul(PS[b][:, :], lhsT=HT[:, k, s * 128:(s + 1) * 128],
                                               rhs=WIN[:, k, col0:col0 + 512], start=(k == 0), stop=(k == 7))
                                      for k in range(8)],
                 reads=[f"HT{s}", f"WIN{G}"], writes=[f"P{b}"])
            return b

        def rope(b, tpar, par, is_q, h):
            t1, t2 = T1[tpar], T2[tpar]
            n1, n2 = f"T1_{tpar}", f"T2_{tpar}"
            S.op("vector", lambda e: e.tensor_tensor(out=t1[:, :], in0=PS[b][:, :], in1=CT[:, :], op=ALU.mult),
                 reads=[f"P{b}", "CT"], writes=[n1])
            S.op("vector", lambda e: [e.tensor_tensor(out=t2[0:64, :], in0=PS[b][64:128, :], in1=ST[64:128, :], op=ALU.mult),
                                      e.tensor_tensor(out=t2[64:128, :], in0=PS[b][0:64, :], in1=ST[0:64, :], op=ALU.mult)],
                 reads=[f"P{b}", "ST"], writes=[n2])
            if is_q:
                S.op("vector", lambda e: e.tensor_tensor(out=QT[par][:, :], in0=t1[:, :], in1=t2[:, :], op=ALU.add),
                     reads=[n1, n2], writes=[f"QT{par}"])
                S.op("gpsimd", lambda e: e.tensor_tensor(out=t1[:, :], in0=t1[:, :], in1=t2[:, :], op=ALU.add),
                     reads=[n1, n2], writes=[n1])
                S.op("gpsimd", lambda e: e.tensor_tensor(out=QDT[par][:, :].rearrange("p (s i) -> p s i", s=NSUB),
                                                         in0=t1[:, :].rearrange("p (s i) -> p s i", s=NSUB),
                                                         in1=QDEC[:, h:h + 1, :].to_broadcast([128, NSUB, 128]),
                                                         op=ALU.mult),
                     reads=[n1, "QDEC"], writes=[f"QDT{par}"])
            else:
                S.op("vector", lambda e: e.tensor_tensor(out=KT[par][:, :], in0=t1[:, :], in1=t2[:, :], op=ALU.add),
                     reads=[n1, n2], writes=[f"KT{par}"])

        def st_T(h):
            par = h % 2
            kt, kd = KT[par], KD[par]
            S.op("tensor", lambda e: [e.transpose(out=psb(4)[:, s * 128:(s + 1) * 128], in_=kt[:, s * 128:(s + 1) * 128],
                                                  identity=IDENT[:, :]) for s in range(NSUB)],
                 reads=[f"KT{par}", "IDENT"], writes=["P4"])
            S.op("scalar", lambda e: e.activation(out=kd[:, :, :].rearrange("p s d -> p (s d)"), in_=psb(4)[:, 0:512],
                                                  func=AF.Copy, scale=COLS[:, KDC + h:KDC + h + 1]),
                 reads=["P4", "COLS"], writes=[f"KD{par}"])

        def st_S(h):
            par = h % 2
            qt, kt, stm = QT[par], KT[par], STM[par]
            S.op("tensor", lambda e: [e.matmul(PS[3][:, s * 128:(s + 1) * 128], lhsT=kt[:, s * 128:(s + 1) * 128],
                                               rhs=qt[:, s * 128:(s + 1) * 128], start=True, stop=True) for s in range(NSUB)],
                 reads=[f"KT{par}", f"QT{par}"], writes=["P3"])
            S.op("vector", lambda e: e.tensor_tensor(out=stm[:, :, :], in0=PS[3][:, :].rearrange("p (s i) -> p s i", s=NSUB),
                                                     in1=MASKT[:, h:h + 1, :].to_broadcast([128, NSUB, 128]), op=ALU.mult),
                 reads=["P3", "MASKT"], writes=[f"STM{par}"])

        def st_KV(h):
            par = h % 2
            kd = KD[par]
            S.op("tensor", lambda e: [e.matmul(PS[5][:, s * 128:(s + 1) * 128], lhsT=kd[:, s, :],
                                               rhs=V[:, s, h * 128:(h + 1) * 128], start=True, stop=True) for s in range(NSUB)],
                 reads=[f"KD{par}"] + [f"V{s}" for s in range(NSUB)], writes=["P5"])
            for s in range(NSUB):
                a, bq = s % 2, (s + 1) % 2
                S.op("vector", lambda e, s=s, a=a, bq=bq: e.scalar_tensor_tensor(
                    out=PST[:, h, bq, :], in0=PST[:, h, a, :], scalar=_CD[h], in1=PS[5][:, s * 128:(s + 1) * 128],
                    op0=ALU.mult, op1=ALU.add),
                    reads=[f"PST{h}_{a}", "P5"], writes=[f"PST{h}_{bq}"])
                if s < NSUB - 1:
                    S.op("scalar", lambda e, s=s, bq=bq: e.activation(out=PBH[:, h, s + 1, :], in_=PST[:, h, bq, :], func=AF.Copy),
                         reads=[f"PST{h}_{bq}"], writes=[f"PBH{h}_{s + 1}"])

        def st_RET(h):
            par = h % 2
            qdt, stm, rn = QDT[par], STM[par], RN[par]

            def retmm(e):
                out = []
                for s in range(NSUB):
                    out.append(e.matmul(PS[6][:, s * 128:(s + 1) * 128], lhsT=stm[:, s, :],
                                        rhs=V[:, s, h * 128:(h + 1) * 128], start=True, stop=False))
                    out.append(e.matmul(PS[6][:, s * 128:(s + 1) * 128], lhsT=qdt[:, s * 128:(s + 1) * 128],
                                        rhs=PBH[:, h, s, :], start=False, stop=True))
                return out
            S.op("tensor", retmm, reads=[f"STM{par}", f"QDT{par}"] + [f"V{s}" for s in range(NSUB)] +
                 [f"PBH{h}_{s}" for s in range(NSUB)], writes=["P6"])
            S.op("scalar", lambda e: e.activation(out=PBH[:, h, 0, :], in_=PST[:, h, 0, :], func=AF.Copy),
                 reads=[f"PST{h}_0"], writes=[f"PBH{h}_0"])
            S.op("vector", lambda e: [e.bn_stats(out=BNS[:, s, :], in_=PS[6][:, s * 128:(s + 1) * 128]) for s in range(NSUB)],
                 reads=["P6"], writes=["BNS"])
            S.op("vector", lambda e: [e.bn_aggr(out=MV[:, s, :], in_=BNS[:, s, :]) for s in range(NSUB)],
                 reads=["BNS"], writes=["MV"])
            norm_coefs(STAT[:, 16:20], MV[:, 0:4, 1], STAT[:, 20:24], MV[:, 0:4, 0], 4, ["MV"], "rsr", "nmr")
            S.op("scalar", lambda e: [e.activation(out=rn[:, s, :], in_=PS[6][:, s * 128:(s + 1) * 128], func=AF.Identity,
                                                   scale=STAT[:, 16 + s:17 + s], bias=STAT[:, 20 + s:21 + s])
                                      for s in range(NSUB)],
                 reads=["P6", "rsr", "nmr"], writes=[f"RN{par}"])

        def st_B(h):
            par = h % 2
            rn = RN[par]
            S.op("tensor", lambda e: [e.transpose(out=psb(7)[:, s * 128:(s + 1) * 128], in_=rn[:, s, :],
                                                  identity=IDENT[:, :]) for s in range(NSUB)],
                 reads=[f"RN{par}", "IDENT"], writes=["P7"])
            S.op("vector", lambda e: e.tensor_tensor(out=MIXT[:, h, :], in0=psb(7)[:, 0:512], in1=SGT[par][:, :], op=ALU.mult),
                 reads=["P7", f"SGT{par}"], writes=[f"MIXT{h}"])

        def proj_qk(h):
            par = h % 2
            bk = win_fm(512 + h * 128, 1)
            rope(bk, 1, par, False, h)
            bq = win_fm(h * 128, 0)
            rope(bq, 0, par, True, h)

        def proj_g(h):
            par = h % 2
            bg = win_fm(1536 + h * 128, 3)
            S.op("scalar", lambda e: e.activation(out=SG[par][:, :], in_=PS[bg][:, :], func=AF.Silu),
                 reads=[f"P{bg}"], writes=[f"SG{par}"])
            S.op("gpsimd", lambda e: e.tensor_scalar(out=SGT[par][:, :], in0=SG[par][:, :],
                                                     scalar1=COLS[:, GNC + h:GNC + h + 1], scalar2=1.0,
                                                     op0=ALU.mult, op1=ALU.mult),
                 reads=[f"SG{par}", "COLS"], writes=[f"SGT{par}"])

        def proj_vg(s):
            b = win_tm(2560, 5, s)
            gv = GV[s % 2]
            gn = f"GV{s % 2}"
            S.op("scalar", lambda e: e.activation(out=gv[:, :], in_=PS[b][:, :], func=AF.Gelu_apprx_tanh),
                 reads=[f"P{b}"], writes=[gn])
            S.op("vector", lambda e: e.bn_stats(out=BNS[:, 4 + s, :], in_=gv[:, :]), reads=[gn], writes=[f"BNSg{s}"])
            S.op("vector", lambda e: e.bn_aggr(out=MV[:, 4 + s, :], in_=BNS[:, 4 + s, :]), reads=[f"BNSg{s}"], writes=[f"MVg{s}"])
            norm_coefs(STAT[:, 24 + s:25 + s], MV[:, 4 + s, 1:2], STAT[:, 28 + s:29 + s], MV[:, 4 + s, 0:1], 1,
                       [f"MVg{s}"], f"rsg{s}", f"nmg{s}")
            S.op("scalar", lambda e: e.activation(out=NN[:, s, :], in_=gv[:, :], func=AF.Identity,
                                                  scale=STAT[:, 24 + s:25 + s], bias=STAT[:, 28 + s:29 + s]),
                 reads=[gn, f"rsg{s}", f"nmg{s}"], writes=[f"NN{s}"])

        def proj_u(g):
            par = g % 2
            bu = win_fm(2048 + g * 128, 4)
            S.op("scalar", lambda e: e.activation(out=GUT[par][:, :], in_=PS[bu][:, :], func=AF.Gelu_apprx_tanh),
                 reads=[f"P{bu}"], writes=[f"GUT{par}"])

        def st_MM(g):
            par = g % 2
            S.op("tensor", lambda e: [e.matmul(PS[7][:, s * 128:(s + 1) * 128], lhsT=NN[:, s, g * 128:(g + 1) * 128],
                                               rhs=WMT[:, g, :], start=True, stop=True) for s in range(NSUB)],
                 reads=[f"NN{s}" for s in range(NSUB)] + ["WMT"], writes=["P7"])
            S.op("vector", lambda e: e.scalar_tensor_tensor(
                out=T1[0][:, :].rearrange("p (s t) -> p s t", s=NSUB), in0=PS[7][:, :].rearrange("p (s t) -> p s t", s=NSUB),
                scalar=COLS[:, LGC + g:LGC + g + 1], in1=B2[:, g:g + 1, :].to_broadcast([128, NSUB, 128]),
                op0=ALU.mult, op1=ALU.add),
                reads=["P7", "COLS", "B2"], writes=["T1_0"])
            S.op("gpsimd", lambda e: e.tensor_tensor(out=MIXT[:, 4 + g, :], in0=T1[0][:, :], in1=GUT[par][:, :], op=ALU.mult),
                 reads=["T1_0", f"GUT{par}"], writes=[f"MIXT{4 + g}"])

        def wo_mm(s, wo_slots):
            b0, b1 = (0, 1) if s % 2 == 0 else (2, 3)

            order = [0, 1, 2, 4, 5, 6, 3, 7]

            def womm(e, cs):
                out = []
                for hf, bb in ((0, b0), (1, b1)):
                    for c in cs:
                        slot = wo_slots[c // 4]
                        out.append(e.matmul(PS[bb][:, :], lhsT=MIXT[:, c, s * 128:(s + 1) * 128],
                                            rhs=RING[slot][:, (c % 4) * 1024 + hf * 512:(c % 4) * 1024 + hf * 512 + 512],
                                            start=(c == order[0]), stop=(c == order[-1])))
                return out
            S.op("tensor", lambda e: womm(e, order[:5]), reads=[f"MIXT{c}" for c in order[:5]] + [f"RING{j}" for j in wo_slots],
                 writes=[f"P{b0}", f"P{b1}"])
            S.op("tensor", lambda e: womm(e, order[5:]), reads=[f"MIXT{c}" for c in order[5:]] + [f"RING{j}" for j in wo_slots],
                 writes=[f"P{b0}", f"P{b1}"])
            S.op("vector", lambda e: [
                e.tensor_tensor(out=XB[:, s, 0:512], in0=XB[:, s, 0:512], in1=PS[b0][:, :], op=ALU.add),
                e.tensor_tensor(out=XB[:, s, 512:1024], in0=XB[:, s, 512:1024], in1=PS[b1][:, :], op=ALU.add)],
                reads=[f"XB{s}", f"P{b0}", f"P{b1}"], writes=[f"XB{s}"])
            xs = XS[s % 2]
            xn = f"XS{s % 2}"
            S.op("scalar", lambda e: e.activation(out=xs[:, :], in_=XB[:, s, :], func=AF.Square,
                                                  accum_out=STAT[:, 8 + s:9 + s]),
                 reads=[f"XB{s}"], writes=[xn, f"ss2_{s}"])
            pow_rstd(STAT[:, 12 + s:13 + s], STAT[:, 8 + s:9 + s], 1, [f"ss2_{s}"], [f"rs2_{s}"], scale=1.0 / D)
            S.op("scalar", lambda e: e.activation(out=xs[:, :], in_=XB[:, s, :], func=AF.Copy,
                                                  scale=STAT[:, 12 + s:13 + s]),
                 reads=[f"XB{s}", f"rs2_{s}"], writes=[xn])

        def wo_tr(s):
            xs = XS[s % 2]
            xn = f"XS{s % 2}"
            S.op("tensor", lambda e: [e.transpose(out=psb(4)[:, k * 128:(k + 1) * 128], in_=xs[:, k * 128:(k + 1) * 128],
                                                  identity=IDENT[:, :]) for k in range(8)],
                 reads=[xn, "IDENT"], writes=["P4"])
            S.op("vector", lambda e: e.tensor_tensor(out=H2T[:, :, s * 128:(s + 1) * 128],
                                                     in0=psb(4).rearrange("p (k t) -> p k t", k=8),
                                                     in1=COLS[:, G2C:G2C + 8].unsqueeze(2).to_broadcast([128, 8, 128]),
                                                     op=ALU.mult),
                 reads=["P4", "COLS"], writes=[f"H2T{s}"])

        prepB(0, 0)
        prepA(0, 2, True)
        prepB(0, 1)
        prepA(0, 3, True)
        prepB(0, 2)
        prepB(0, 3)

        for i in range(NT):
            wo_slots = [ring_load(f"wo{b}", 4096) for b in range(2)]
            if i > 0:
                for s in range(NSUB):
                    S.op("sync", lambda e, s=s, i=i: e.dma_start(out=XB[:, s, :], in_=x_t[i, s]), writes=[f"XB{s}"], dma=f"xb_{s}")

            for s in range(NSUB):
                b = win_tm(1024, 2, s)
                S.op("scalar", lambda e, s=s, b=b: e.activation(out=V[:, s, :], in_=PS[b][:, :], func=AF.Copy),
                     reads=[f"P{b}"], writes=[f"V{s}"])
            proj_qk(0)
            proj_g(0)
            proj_vg(0)
            st_T(0)
            proj_vg(1)
            for h in range(4):
                st_S(h)
                st_KV(h)
                if h < 3:
                    proj_qk(h + 1)
                else:
                    proj_vg(2)
                    proj_vg(3)
                if h >= 1:
                    st_B(h - 1)
                if h < 3:
                    proj_g(h + 1)
                    st_T(h + 1)
                else:
                    proj_u(0)
                st_RET(h)
            proj_u(1)
            st_MM(0)
            proj_u(2)
            st_MM(1)
            proj_u(3)
            st_MM(2)
            st_B(3)
            st_MM(3)
            if i + 1 < NT:
                S.op("sync", lambda e, i=i: e.dma_start(out=CT[:, :], in_=ct_d[i + 1]), writes=["CT"], dma="t_c")
                S.op("sync", lambda e, i=i: e.dma_start(out=ST[:, :], in_=st_d[i + 1]), writes=["ST"], dma="t_s")

            wo_mm(0, wo_slots)
            for s in range(1, NSUB):
                wo_mm(s, wo_slots)
                wo_tr(s - 1)

            prep_next = 0
            for hf in range(2):
                gu_blocks = list(range(0, HALF, 2))
                dn_blocks = list(range(0, HALF, 4))
                gu_slot = {}
                pend = [("gu", b) for b in gu_blocks] + [("dn", b) for b in dn_blocks]
                loaded = {}
                nxt = 0

                def ensure(upto):
                    nonlocal nxt
                    while nxt < len(pend) and nxt <= upto:
                        kind, b = pend[nxt]
                        n = min(2 if kind == "gu" else 4, HALF - b)
                        ncols = n * (2048 if kind == "gu" else 1024)
                        loaded[(kind, b)] = ring_load(f"{kind}{hf}_{b}", ncols)
                        nxt += 1

                late = []

                def evac_gu(pb, cl):
                    sg = SG[cl % 2]
                    S.op("scalar", lambda e: e.activation(out=sg[:, :], in_=PS[pb][:, :], func=AF.Silu),
                         reads=[f"P{pb}"], writes=[f"SG{cl % 2}"])
                    S.op("vector", lambda e: e.tensor_tensor(out=HID[:, cl, :], in0=sg[:, :], in1=PS[pb + 1][:, :], op=ALU.mult),
                         reads=[f"SG{cl % 2}", f"P{pb + 1}"], writes=[f"HID{cl}"])

                ensure(NRING - 3)
                for cl in range(HALF):
                    blk = (cl // 2) * 2
                    ensure(gu_blocks.index(blk) + NRING - 2)
                    slot = loaded[("gu", blk)]
                    off = (cl - blk) * 2048
                    pb = 0 if cl % 2 == 0 else 2
                    def gumm(e, slot=slot, off=off, pb=pb, c0=0, c1=T):
                        return ([e.matmul(PS[pb][:, c0:c1], lhsT=RING[slot][:, off + k * 128:off + (k + 1) * 128],
                                          rhs=H2T[:, k, c0:c1], start=(k == 0), stop=(k == 7)) for k in range(8)] +
                                [e.matmul(PS[pb + 1][:, c0:c1], lhsT=RING[slot][:, off + 1024 + k * 128:off + 1024 + (k + 1) * 128],
                                          rhs=H2T[:, k, c0:c1], start=(k == 0), stop=(k == 7)) for k in range(8)])
                    if hf == 0 and cl < 2:
                        S.op("tensor", lambda e, g=gumm: g(e, c0=0, c1=384), reads=H2Tall[:3] + [f"RING{slot}"],
                             writes=[f"P{pb}", f"P{pb + 1}"])
                        late.append((gumm, slot, pb, cl))
                        if cl == 1:
                            wo_tr(NSUB - 1)
                            for (g2, sl2, pb2, cl2) in late:
                                S.op("tensor", lambda e, g=g2: g(e, c0=384, c1=512), reads=H2Tall[3:] + [f"RING{sl2}"],
                                     writes=[f"P{pb2}", f"P{pb2 + 1}"])
                                evac_gu(pb2, cl2)
                    else:
                        S.op("tensor", gumm, reads=H2Tall + [f"RING{slot}"], writes=[f"P{pb}", f"P{pb + 1}"])
                        evac_gu(pb, cl)
                    if i + 1 < NT and hf == 0:
                        if cl in (1, 3, 5, 7):
                            prepA(i + 1, (cl - 1) // 2)
                        if cl in (4, 6, 8, 10):
                            prepB(i + 1, (cl - 4) // 2)
                for cl in range(HALF):
                    blk = (cl // 4) * 4
                    ensure(len(gu_blocks) + dn_blocks.index(blk) + 1)
                    slot = loaded[("dn", blk)]
                    off = (cl - blk) * 1024
                    S.op("tensor", lambda e, slot=slot, off=off, cl=cl: [
                        e.matmul(PS[s * 2 + h2][:, :], lhsT=HID[:, cl, s * 128:(s + 1) * 128],
                                 rhs=RING[slot][:, off + h2 * 512:off + h2 * 512 + 512],
                                 start=(cl == 0), stop=(cl == HALF - 1))
                        for s in range(NSUB) for h2 in range(2)],
                        reads=([f"HID{c}" for c in range(HALF)] if cl == 0 else [f"HID{cl}"]) + [f"RING{slot}"],
                        writes=[f"P{j}" for j in range(8)])
                for s in range(NSUB):
                    S.op("vector", lambda e, s=s: [
                        e.tensor_tensor(out=XB[:, s, 0:512], in0=XB[:, s, 0:512], in1=PS[2 * s][:, :], op=ALU.add),
                        e.tensor_tensor(out=XB[:, s, 512:1024], in0=XB[:, s, 512:1024], in1=PS[2 * s + 1][:, :], op=ALU.add)],
                        reads=[f"XB{s}", f"P{2 * s}", f"P{2 * s + 1}"], writes=[f"XB{s}"])

            for s in range(NSUB):
                xs = XS[s % 2]
                xn = f"XS{s % 2}"
                S.op("scalar", lambda e, s=s, xs=xs: e.activation(out=xs[:, :], in_=XB[:, s, :], func=AF.Square,
                                                                  accum_out=STAT[:, 32 + s:33 + s]),
                     reads=[f"XB{s}"], writes=[xn, f"ss3_{s}"])
                pow_rstd(STAT[:, 36 + s:37 + s], STAT[:, 32 + s:33 + s], 1, [f"ss3_{s}"], [f"rs3_{s}"], scale=1.0 / D)
                S.op("vector", lambda e, s=s: e.scalar_tensor_tensor(out=XB[:, s, :], in0=XB[:, s, :], scalar=STAT[:, 36 + s:37 + s],
                                                                     in1=FGB[:, :], op0=ALU.mult, op1=ALU.mult),
                     reads=[f"XB{s}", f"rs3_{s}", "FGB"], writes=[f"XB{s}"])
                S.op("sync", lambda e, s=s, i=i: e.dma_start(out=y_t[i, s], in_=XB[:, s, :]), reads=[f"XB{s}"], dma=f"y_{s}")

        S.final_wait("sync", [f"y_{s}" for s in range(NSUB)])
        S.emit()
    return nc


def _prep_shared(norm1_g, w_in, ret_gn_g, gmlp_ln_g, gmlp_ln_b, w_s, b_s, w_out, norm2_g,
                 w_ffn_gate, w_ffn_up, w_ffn_down, final_g):
    c = _consts()
    f = np.float32
    w_in_l = np.ascontiguousarray(np.asarray(w_in[0], f).reshape(8, 128, 3072).transpose(1, 0, 2))
    w_out_l = np.ascontiguousarray(np.asarray(w_out[0], f).reshape(8, 128, 1024).transpose(1, 0, 2))
    wg = np.asarray(w_ffn_gate[0], f).reshape(8, 128, NFC, 128)
    wu = np.asarray(w_ffn_up[0], f).reshape(8, 128, NFC, 128)
    gu = np.stack([wg, wu], axis=0)
    w_gu_l = np.ascontiguousarray(gu.transpose(2, 3, 0, 1, 4).reshape(128, NFC, 2048))
    w_dn_l = np.ascontiguousarray(np.asarray(w_ffn_down[0], f).reshape(NFC, 128, 1024).transpose(1, 0, 2))
    cols = np.zeros((128, 32), f)
    cols[:, 0:8] = np.asarray(norm1_g[0], f).reshape(8, 128).T
    cols[:, 8:16] = np.asarray(norm2_g[0], f).reshape(8, 128).T
    cols[:, 16:20] = np.asarray(ret_gn_g[0], f).reshape(4, 128).T
    cols[:, 20:24] = np.asarray(gmlp_ln_g[0], f).reshape(4, 128).T
    cols[:, 24:28] = np.asarray(gmlp_ln_b[0], f).reshape(4, 128).T
    cols[:, 28:32] = c["kdec"]
    bbc = np.ascontiguousarray(np.broadcast_to(np.asarray(b_s[0], f)[None, :, :], (128, 4, 128)))
    wsT = np.ascontiguousarray(np.asarray(w_s[0], f).transpose(2, 0, 1))
    fgb = np.ascontiguousarray(np.broadcast_to(np.asarray(final_g, f)[None, :], (128, 1024)))
    return {
        "w_in_l": w_in_l, "w_out_l": w_out_l, "w_gu_l": w_gu_l, "w_dn_l": w_dn_l,
        "ct": c["ct"], "st": c["st"], "maskT": c["maskT"], "qdec": c["qdec"],
        "bbc": bbc, "wsT": wsT, "fgb": fgb, "cols": cols, "ident": np.eye(128, dtype=f),
    }


def kernel(x, norm1_g, w_in, ret_gn_g, gmlp_ln_g, gmlp_ln_b, w_s, b_s, w_out,
           norm2_g, w_ffn_gate, w_ffn_up, w_ffn_down, final_g):
    x = np.asarray(x, np.float32)
    shared = _prep_shared(norm1_g, w_in, ret_gn_g, gmlp_ln_g, gmlp_ln_b, w_s, b_s, w_out, norm2_g,
                          w_ffn_gate, w_ffn_up, w_ffn_down, final_g)
    nc = build_program(NT_FULL)
    in_maps = []
    for b in range(8):
        m = dict(shared)
        m["x"] = np.ascontiguousarray(x[b])
        in_maps.append(m)
    res = run_bass_kernel_spmd(nc, in_maps, core_ids=list(range(8)))
    return np.stack([np.asarray(r["y"], np.float32) for r in res.results], axis=0)
```

```python
import numpy as np
from contextlib import ExitStack
import concourse.bass as bass
import concourse.mybir as mybir
from concourse.bass_utils import run_bass_kernel_spmd

F32 = mybir.dt.float32
BF16 = mybir.dt.bfloat16
AF = mybir.ActivationFunctionType
ALU = mybir.AluOpType

D = 1024
SEQ = 4096
T = 512
NSUB = 4
NT_FULL = SEQ // T
DFF = 2816
NFC = DFF // 128
HALF = NFC // 2
EPS = 1e-6
NRING = 4


class Sched:
    ENGS = ("sync", "tensor", "vector", "scalar", "gpsimd")

    def __init__(self, nc, es):
        self.nc = nc
        self.es = es
        self.ops = {e: [] for e in self.ENGS}
        self.sems = {}
        self.count = {}
        self.known = {e: {} for e in self.ENGS}
        self.res = {}
        self.snap = {}
        for e in self.ENGS:
            self._mk(e)

    def _mk(self, name):
        if name not in self.sems:
            self.sems[name] = self.es.enter_context(self.nc.semaphore("s_" + name))
            self.count[name] = 0

    def _r(self, name):
        if name not in self.res:
            self.res[name] = {"w": None, "r": []}
        return self.res[name]

    def op(self, eng, fn, reads=(), writes=(), dma=None, ndma=1):
        need = {}

        def want(idv, raw):
            if idv is None:
                return
            ch, val = idv
            if ch == eng and not raw and eng == "tensor":
                return
            if self.known[eng].get(ch, 0) >= val:
                return
            need[ch] = max(need.get(ch, 0), val)

        for r in reads:
            want(self._r(r)["w"], True)
        for w in writes:
            rr = self._r(w)
            want(rr["w"], False)
            for rd in rr["r"]:
                want(rd, False)
        if dma is not None:
            self._mk(dma)
            self.count[dma] += 16 * ndma
            my = (dma, self.count[dma])
            inc = (dma, 16)
        else:
            self.count[eng] += 1
            my = (eng, self.count[eng])
            inc = (eng, 1)
        implied = {}
        for ch, val in need.items():
            for c2, v2 in self.snap.get((ch, val), {}).items():
                if v2 > implied.get(c2, 0):
                    implied[c2] = v2
        need = {ch: val for ch, val in need.items() if implied.get(ch, 0) < val}
        kn = self.known[eng]
        for ch, val in list(need.items()) + list(implied.items()):
            if val > kn.get(ch, 0):
                kn[ch] = val
        self.snap[my] = dict(kn)
        for r in reads:
            self._r(r)["r"].append(my)
        for w in writes:
            rr = self._r(w)
            rr["w"] = my
            rr["r"] = []
        self.ops[eng].append((list(need.items()), fn, inc, dma is not None, my))
        return my

    def final_wait(self, eng, chans):
        need = [(ch, self.count[ch]) for ch in chans if self.count.get(ch, 0) > 0]
        self.ops[eng].append((need, None, None, False, None))

    def emit(self):
        nc = self.nc

        HOIST = 0

        def hoist_plan(name):
            ops = self.ops[name]
            early = {j: [] for j in range(len(ops))}
            kept = {}
            for j, (waits, fn, inc, is_dma, my) in enumerate(ops):
                kept[j] = list(waits)
                if fn is None or name != "tensor":
                    continue
                keep = []
                for (ch, val) in waits:
                    dep = self.snap.get((ch, val), {}).get(name, 0)
                    p = j
                    while p > 0 and j - p < HOIST:
                        prev = ops[p - 1]
                        if prev[1] is None or prev[4] is None or prev[4][0] != name:
                            break
                        if dep >= prev[4][1]:
                            break
                        p -= 1
                    if p < j:
                        early[p].append((ch, val))
                    else:
                        keep.append((ch, val))
                kept[j] = keep
            return early, kept

        def replay(name, e):
            early, kept = hoist_plan(name)
            for j, (waits, fn, inc, is_dma, my) in enumerate(self.ops[name]):
                for ch, val in early[j]:
                    e.wait_ge(self.sems[ch], val)
                waits = kept[j]
                if fn is None:
                    for ch, val in waits:
                        e.wait_ge(self.sems[ch], val)
                    continue
                for ch, val in waits[1:]:
                    e.wait_ge(self.sems[ch], val)
                r = fn(e)
                ins = r if isinstance(r, (list, tuple)) else [r]
                if waits:
                    ch, val = waits[0]
                    ins[0]._wait_ge(self.sems[ch], val)
                if is_dma:
                    for i_ in ins:
                        i_.then_inc(self.sems[inc[0]], 16)
                else:
                    ins[-1].then_inc(self.sems[inc[0]], 1)

        with nc.Block() as block:
            @block.sync
            def _(e):
                replay("sync", e)

            @block.tensor
            def _(e):
                replay("tensor", e)

            @block.vector
            def _(e):
                replay("vector", e)

            @block.scalar
            def _(e):
                replay("scalar", e)

            @block.gpsimd
            def _(e):
                replay("gpsimd", e)


def _consts():
    inv = (10000.0 ** (-np.arange(0, 128, 2, dtype=np.float32) / np.float32(128))).astype(np.float32)
    pos = np.arange(SEQ, dtype=np.float32)
    ang = (pos[:, None] * inv[None, :]).astype(np.float32)
    cos = np.cos(ang).astype(np.float32).T
    sin = np.sin(ang).astype(np.float32).T
    ct = np.concatenate([cos, cos], axis=0)
    st = np.concatenate([sin, -sin], axis=0)
    ct = np.ascontiguousarray(ct.reshape(128, NT_FULL, T).transpose(1, 0, 2))
    st = np.ascontiguousarray(st.reshape(128, NT_FULL, T).transpose(1, 0, 2))
    gam = 1.0 - 2.0 ** (-5.0 - np.arange(4, dtype=np.float64))
    i = np.arange(128)
    ci = i // 64
    maskT = np.zeros((128, 4, 128), np.float64)
    for h in range(4):
        dif = i[None, :] - i[:, None]
        same = ci[None, :] == ci[:, None]
        later = ci[None, :] > ci[:, None]
        m = np.where(same, gam[h] ** np.abs(dif), np.where(later, gam[h] ** np.maximum(dif, 0), 0.0))
        maskT[:, h, :] = m * (128.0 ** -0.5)
    qdec = np.zeros((128, 4, 128), np.float64)
    kdec = np.zeros((128, 4), np.float64)
    for h in range(4):
        qdec[:, h, :] = (gam[h] ** (i + 1.0))[None, :] * (128.0 ** -0.5)
        kdec[:, h] = gam[h] ** (127.0 - i)
    cd = [float(gam[h] ** 128.0) for h in range(4)]
    return dict(ct=ct, st=st, maskT=maskT.astype(np.float32), qdec=qdec.astype(np.float32),
                kdec=kdec.astype(np.float32), cd=cd)


_CD = [float((1.0 - 2.0 ** (-5.0 - h)) ** 128.0) for h in range(4)]


def build_program(NT=NT_FULL):
    nc = bass.Bass("TRN2", target_bir_lowering=False)

    def din(name, shape):
        return nc.dram_tensor(name, list(shape), F32, kind="ExternalInput").ap()

    x_d = din("x", [NT * T, D])
    win_d = din("w_in_l", [128, 8, 3072])
    wout_d = din("w_out_l", [128, 8, 1024])
    wgu_d = din("w_gu_l", [128, NFC, 2048])
    wdn_d = din("w_dn_l", [128, NFC, 1024])
    ct_d = din("ct", [NT_FULL, 128, T])
    st_d = din("st", [NT_FULL, 128, T])
    maskT_d = din("maskT", [128, 4, 128])
    qdec_d = din("qdec", [128, 4, 128])
    bbc_d = din("bbc", [128, 4, 128])
    wsT_d = din("wsT", [128, 4, 128])
    fgb_d = din("fgb", [128, 1024])
    cols_d = din("cols", [128, 32])
    ident_d = din("ident", [128, 128])
    y_d = nc.dram_tensor("y", [NT * T, D], F32, kind="ExternalOutput").ap()
    wout_s = nc.dram_tensor("wout_s", [128, 8, 1024], BF16, kind="Internal").ap()
    wgu_s = nc.dram_tensor("wgu_s", [128, NFC, 2048], BF16, kind="Internal").ap()
    wdn_s = nc.dram_tensor("wdn_s", [128, NFC, 1024], BF16, kind="Internal").ap()

    x_t = x_d.rearrange("(i s p) d -> i s p d", s=NSUB, p=128)
    y_t = y_d.rearrange("(i s p) d -> i s p d", s=NSUB, p=128)

    es = ExitStack()
    with es:
        def sb(name, shape, dt=F32):
            return es.enter_context(nc.sbuf_tensor(name, list(shape), dt))

        WIN = sb("WIN", [128, 8, 3072], BF16)
        RING = [sb(f"RING{j}", [128, 4096], BF16) for j in range(NRING)]
        XB = sb("XB", [128, NSUB, 1024])
        XSTG = [sb(f"XSTG{j}", [128, 1024]) for j in range(2)]
        HT = sb("HT", [128, 8, T], BF16)
        H2T = sb("H2T", [128, 8, T], BF16)
        XS = [sb(f"XS{j}", [128, 1024], BF16) for j in range(2)]
        MIXT = sb("MIXT", [128, 8, T], BF16)
        HID = sb("HID", [128, HALF, T], BF16)
        SG = [sb(f"SG{j}", [128, T]) for j in range(2)]
        CT = sb("CT", [128, T])
        ST = sb("ST", [128, T])
        T1 = [sb(f"T1_{j}", [128, T]) for j in range(2)]
        T2 = [sb(f"T2_{j}", [128, T]) for j in range(2)]
        QT = [sb(f"QT{j}", [128, T], BF16) for j in range(2)]
        KT = [sb(f"KT{j}", [128, T], BF16) for j in range(2)]
        QDT = [sb(f"QDT{j}", [128, T], BF16) for j in range(2)]
        KD = [sb(f"KD{j}", [128, NSUB, 128], BF16) for j in range(2)]
        V = sb("V", [128, NSUB, 512], BF16)
        SGT = [sb(f"SGT{j}", [128, T], BF16) for j in range(2)]
        GUT = [sb(f"GUT{j}", [128, T], BF16) for j in range(2)]
        NN = sb("NN", [128, NSUB, 512], BF16)
        GV = [sb(f"GV{j}", [128, 512]) for j in range(2)]
        STM = [sb(f"STM{j}", [128, NSUB, 128], BF16) for j in range(2)]
        RN = [sb(f"RN{j}", [128, NSUB, 128], BF16) for j in range(2)]
        PST = sb("PST", [128, 4, 2, 128])
        PBH = sb("PBH", [128, 4, NSUB, 128], BF16)
        FGB = sb("FGB", [128, 1024])
        MASKT = sb("MASKT", [128, 4, 128])
        QDEC = sb("QDEC", [128, 4, 128])
        B2 = sb("B2", [128, 4, 128])
        WMT = sb("WMT", [128, 4, 128], BF16)
        IDENT = sb("IDENT", [128, 128], BF16)
        ONES = sb("ONES", [128, 128], BF16)
        COLS = sb("COLS", [128, 32])
        MHALF = sb("MHALF", [128, 8])
        STAT = sb("STAT", [128, 64])
        BNS = sb("BNS", [128, 8, 6])
        MV = sb("MV", [128, 8, 2])

        PS = [es.enter_context(nc.psum_tensor(f"PS{j}", [128, 512], F32)) for j in range(8)]

        def psb(j):
            return PS[j][:, :].bitcast(BF16)

        S = Sched(nc, es)
        G1C, G2C, GNC, LGC, LBC, KDC = 0, 8, 16, 20, 24, 28

        def pow_rstd(out_ap, in_ap, n, reads, writes, scale=None):
            sc = 1.0 if scale is None else scale
            S.op("vector", lambda e: e.tensor_scalar(out=out_ap, in0=in_ap, scalar1=sc, scalar2=EPS,
                                                     op0=ALU.mult, op1=ALU.add), reads=reads, writes=writes)
            S.op("gpsimd", lambda e: e.tensor_tensor(out=out_ap, in0=out_ap, in1=MHALF[:, 0:n], op=ALU.pow),
                 reads=writes + ["MHALF"], writes=writes)

        def norm_coefs(rs_ap, var_ap, nm_ap, mean_ap, n, reads, rname, nname):
            S.op("vector", lambda e: [e.tensor_scalar(out=rs_ap, in0=var_ap, scalar1=1.0, scalar2=EPS, op0=ALU.mult, op1=ALU.add),
                                      e.tensor_scalar(out=nm_ap, in0=mean_ap, scalar1=-1.0, scalar2=None, op0=ALU.mult)],
                 reads=reads, writes=[rname, nname])
            S.op("gpsimd", lambda e: e.tensor_tensor(out=rs_ap, in0=rs_ap, in1=MHALF[:, 0:n], op=ALU.pow),
                 reads=[rname, "MHALF"], writes=[rname])
            S.op("gpsimd", lambda e: e.tensor_tensor(out=nm_ap, in0=nm_ap, in1=rs_ap, op=ALU.mult),
                 reads=[rname, nname], writes=[nname])

        def prepA(i, s, from_xb=False):
            xs = XS[s % 2]
            xn = f"XS{s % 2}"
            if from_xb:
                stg_ap = XB[:, s, :]
                sn = f"XB{s}"
            else:
                stg_ap = XSTG[s % 2][:, :]
                sn = f"XSTG{s % 2}"
                S.op("sync", lambda e: e.dma_start(out=stg_ap, in_=x_t[i, s]), writes=[sn], dma=f"x_{s % 2}")
            S.op("scalar", lambda e: e.activation(out=xs[:, :], in_=stg_ap, func=AF.Square,
                                                  accum_out=STAT[:, s:s + 1]),
                 reads=[sn], writes=[xn, f"ss1_{s}"])
            pow_rstd(STAT[:, 4 + s:5 + s], STAT[:, s:s + 1], 1, [f"ss1_{s}"], [f"rs1_{s}"], scale=1.0 / D)
            S.op("scalar", lambda e: e.activation(out=xs[:, :], in_=stg_ap, func=AF.Copy,
                                                  scale=STAT[:, 4 + s:5 + s]),
                 reads=[sn, f"rs1_{s}"], writes=[xn])

        def prepB(i, s):
            xs = XS[s % 2]
            xn = f"XS{s % 2}"
            S.op("tensor", lambda e: [e.transpose(out=psb(4)[:, k * 128:(k + 1) * 128], in_=xs[:, k * 128:(k + 1) * 128],
                                                  identity=IDENT[:, :]) for k in range(8)],
                 reads=[xn, "IDENT"], writes=["P4"])
            S.op("vector", lambda e: e.tensor_tensor(out=HT[:, :, s * 128:(s + 1) * 128],
                                                     in0=psb(4).rearrange("p (k t) -> p k t", k=8),
                                                     in1=COLS[:, G1C:G1C + 8].unsqueeze(2).to_broadcast([128, 8, 128]),
                                                     op=ALU.mult),
                 reads=["P4", "COLS"], writes=[f"HT{s}"])

        def ld(eng, dst, src, sem, reads=(), writes=()):
            S.op(eng, lambda e: e.dma_start(out=dst, in_=src), reads=reads, writes=writes, dma=sem)

        S.op("vector", lambda e: e.memset(MHALF[:, :], -0.5), writes=["MHALF"])
        ld("sync", COLS[:, :], cols_d, "c_cols", writes=["COLS"])
        for s in range(NSUB):
            S.op("gpsimd", lambda e, s=s: e.dma_start(out=XB[:, s, :], in_=x_t[0, s]), writes=[f"XB{s}"], dma=f"xb_{s}")
        ld("gpsimd", IDENT[:, :], ident_d, "c_id", writes=["IDENT"])
        prepA(0, 0, True)
        prepA(0, 1, True)
        for G in (2, 1, 0, 3, 5, 4):
            ld("gpsimd", WIN[:, :, G * 512:(G + 1) * 512], win_d[:, :, G * 512:(G + 1) * 512],
               f"c_win{G}", writes=[f"WIN{G}"])
        ld("sync", CT[:, :], ct_d[0], "t_c", writes=["CT"])
        ld("sync", ST[:, :], st_d[0], "t_s", writes=["ST"])
        ld("sync", MASKT[:, :, :], maskT_d, "c_mask", writes=["MASKT"])
        ld("sync", QDEC[:, :, :], qdec_d, "c_qdec", writes=["QDEC"])
        ld("sync", T1[0][:, :], bbc_d.rearrange("p g t -> p (g t)"), "c_bbc", writes=["T1_0"])
        ld("sync", T2[0][:, :], wsT_d.rearrange("p g t -> p (g t)"), "c_wst", writes=["T2_0"])
        ld("sync", FGB[:, :], fgb_d, "c_fgb", writes=["FGB"])
        stream = []
        for b in range(2):
            stream.append((f"wo{b}", wout_s[:, 4 * b:4 * b + 4, :], wout_d[:, 4 * b:4 * b + 4, :]))
        for hf in range(2):
            c0 = hf * HALF
            for b in range(0, HALF, 2):
                n = min(2, HALF - b)
                stream.append((f"gu{hf}_{b}", wgu_s[:, c0 + b:c0 + b + n, :], wgu_d[:, c0 + b:c0 + b + n, :]))
            for b in range(0, HALF, 4):
                n = min(4, HALF - b)
                stream.append((f"dn{hf}_{b}", wdn_s[:, c0 + b:c0 + b + n, :], wdn_d[:, c0 + b:c0 + b + n, :]))
        for (nm, dst, src) in stream:
            ld("gpsimd", dst, src, "cv_" + nm, writes=["scr_" + nm])

        S.op("vector", lambda e: e.memset(ONES[:, :], 1.0), writes=["ONES"])
        S.op("vector", lambda e: e.memset(PST[:, :, :, :].rearrange("p a b c -> p (a b c)"), 0.0), writes=[f"PST{h}_0" for h in range(4)] + [f"PST{h}_1" for h in range(4)])
        S.op("vector", lambda e: e.memset(PBH[:, :, :, :].rearrange("p a b c -> p (a b c)"), 0.0), writes=[f"PBH{h}_{s}" for h in range(4) for s in range(NSUB)])
        S.op("vector", lambda e: e.tensor_copy(out=WMT[:, :, :].rearrange("p g t -> p (g t)"), in_=T2[0][:, :]),
             reads=["T2_0"], writes=["WMT"])
        S.op("vector", lambda e: e.memset(WMT[64:128, :, 0:64], 0.0), reads=["WMT"], writes=["WMT"])
        S.op("tensor", lambda e: e.matmul(PS[0][:, :], lhsT=ONES[:, :], rhs=WMT[:, :, :].rearrange("p g t -> p (g t)"),
                                          start=True, stop=True), reads=["ONES", "WMT"], writes=["P0"])

        def b2f(e):
            out = []
            for g in range(4):
                out.append(e.scalar_tensor_tensor(out=B2[:, g, :], in0=PS[0][:, g * 128:(g + 1) * 128],
                                                  scalar=COLS[:, LBC + g:LBC + g + 1], in1=T1[0][:, g * 128:(g + 1) * 128],
                                                  op0=ALU.mult, op1=ALU.add))
            return out
        S.op("vector", b2f, reads=["P0", "COLS", "T1_0"], writes=["B2"])

        ring_ctr = [0]

        def ring_load(nm, ncols):
            j = ring_ctr[0] % NRING
            ring_ctr[0] += 1
            src = dict((a, b) for a, b, _ in stream)[nm]
            src2 = src.rearrange("p c n -> p (c n)")
            S.op("sync", lambda e: e.dma_start(out=RING[j][:, 0:ncols], in_=src2),
                 reads=["scr_" + nm], writes=[f"RING{j}"], dma=f"r_{j}")
            return j

        HTall = [f"HT{s}" for s in range(NSUB)]
        H2Tall = [f"H2T{s}" for s in range(NSUB)]
        pbank = [0]

        def big_bank():
            b = pbank[0] % 3
            pbank[0] += 1
            return b

        def win_fm(col0, G):
            b = big_bank()
            S.op("tensor", lambda e: [e.matmul(PS[b][:, :], lhsT=WIN[:, k, col0:col0 + 128], rhs=HT[:, k, :],
                                               start=(k == 0), stop=(k == 7)) for k in range(8)],
                 reads=HTall + [f"WIN{G}"], writes=[f"P{b}"])
            return b

        def win_tm(col0, G, s):
            b = big_bank()
            S.op("tensor", lambda e: [e.matmul(PS[b][:, :], lhsT=HT[:, k, s * 128:(s + 1) * 128],
                                               rhs=WIN[:, k, col0:col0 + 512], start=(k == 0), stop=(k == 7))
                                      for k in range(8)],
                 reads=[f"HT{s}", f"WIN{G}"], writes=[f"P{b}"])
            return b

        def rope(b, tpar, par, is_q, h):
            t1, t2 = T1[tpar], T2[tpar]
            n1, n2 = f"T1_{tpar}", f"T2_{tpar}"
            S.op("vector", lambda e: e.tensor_tensor(out=t1[:, :], in0=PS[b][:, :], in1=CT[:, :], op=ALU.mult),
                 reads=[f"P{b}", "CT"], writes=[n1])
            S.op("vector", lambda e: [e.tensor_tensor(out=t2[0:64, :], in0=PS[b][64:128, :], in1=ST[64:128, :], op=ALU.mult),
                                      e.tensor_tensor(out=t2[64:128, :], in0=PS[b][0:64, :], in1=ST[0:64, :], op=ALU.mult)],
                 reads=[f"P{b}", "ST"], writes=[n2])
            if is_q:
                S.op("vector", lambda e: e.tensor_tensor(out=QT[par][:, :], in0=t1[:, :], in1=t2[:, :], op=ALU.add),
                     reads=[n1, n2], writes=[f"QT{par}"])
                S.op("gpsimd", lambda e: e.tensor_tensor(out=t1[:, :], in0=t1[:, :], in1=t2[:, :], op=ALU.add),
                     reads=[n1, n2], writes=[n1])
                S.op("gpsimd", lambda e: e.tensor_tensor(out=QDT[par][:, :].rearrange("p (s i) -> p s i", s=NSUB),
                                                         in0=t1[:, :].rearrange("p (s i) -> p s i", s=NSUB),
                                                         in1=QDEC[:, h:h + 1, :].to_broadcast([128, NSUB, 128]),
                                                         op=ALU.mult),
                     reads=[n1, "QDEC"], writes=[f"QDT{par}"])
            else:
                S.op("vector", lambda e: e.tensor_tensor(out=KT[par][:, :], in0=t1[:, :], in1=t2[:, :], op=ALU.add),
                     reads=[n1, n2], writes=[f"KT{par}"])

        def st_T(h):
            par = h % 2
            kt, kd = KT[par], KD[par]
            S.op("tensor", lambda e: [e.transpose(out=psb(4)[:, s * 128:(s + 1) * 128], in_=kt[:, s * 128:(s + 1) * 128],
                                                  identity=IDENT[:, :]) for s in range(NSUB)],
                 reads=[f"KT{par}", "IDENT"], writes=["P4"])
            S.op("scalar", lambda e: e.activation(out=kd[:, :, :].rearrange("p s d -> p (s d)"), in_=psb(4)[:, 0:512],
                                                  func=AF.Copy, scale=COLS[:, KDC + h:KDC + h + 1]),
                 reads=["P4", "COLS"], writes=[f"KD{par}"])

        def st_S(h):
            par = h % 2
            qt, kt, stm = QT[par], KT[par], STM[par]
            S.op("tensor", lambda e: [e.matmul(PS[3][:, s * 128:(s + 1) * 128], lhsT=kt[:, s * 128:(s + 1) * 128],
                                               rhs=qt[:, s * 128:(s + 1) * 128], start=True, stop=True) for s in range(NSUB)],
                 reads=[f"KT{par}", f"QT{par}"], writes=["P3"])
            S.op("vector", lambda e: e.tensor_tensor(out=stm[:, :, :], in0=PS[3][:, :].rearrange("p (s i) -> p s i", s=NSUB),
                                                     in1=MASKT[:, h:h + 1, :].to_broadcast([128, NSUB, 128]), op=ALU.mult),
                 reads=["P3", "MASKT"], writes=[f"STM{par}"])

        def st_KV(h):
            par = h % 2
            kd = KD[par]
            S.op("tensor", lambda e: [e.matmul(PS[5][:, s * 128:(s + 1) * 128], lhsT=kd[:, s, :],
                                               rhs=V[:, s, h * 128:(h + 1) * 128], start=True, stop=True) for s in range(NSUB)],
                 reads=[f"KD{par}"] + [f"V{s}" for s in range(NSUB)], writes=["P5"])
            for s in range(NSUB):
                a, bq = s % 2, (s + 1) % 2
                S.op("vector", lambda e, s=s, a=a, bq=bq: e.scalar_tensor_tensor(
                    out=PST[:, h, bq, :], in0=PST[:, h, a, :], scalar=_CD[h], in1=PS[5][:, s * 128:(s + 1) * 128],
                    op0=ALU.mult, op1=ALU.add),
                    reads=[f"PST{h}_{a}", "P5"], writes=[f"PST{h}_{bq}"])
                if s < NSUB - 1:
                    S.op("scalar", lambda e, s=s, bq=bq: e.activation(out=PBH[:, h, s + 1, :], in_=PST[:, h, bq, :], func=AF.Copy),
                         reads=[f"PST{h}_{bq}"], writes=[f"PBH{h}_{s + 1}"])

        def st_RET(h):
            par = h % 2
            qdt, stm, rn = QDT[par], STM[par], RN[par]

            def retmm(e):
                out = []
                for s in range(NSUB):
                    out.append(e.matmul(PS[6][:, s * 128:(s + 1) * 128], lhsT=stm[:, s, :],
                                        rhs=V[:, s, h * 128:(h + 1) * 128], start=True, stop=False))
                    out.append(e.matmul(PS[6][:, s * 128:(s + 1) * 128], lhsT=qdt[:, s * 128:(s + 1) * 128],
                                        rhs=PBH[:, h, s, :], start=False, stop=True))
                return out
            S.op("tensor", retmm, reads=[f"STM{par}", f"QDT{par}"] + [f"V{s}" for s in range(NSUB)] +
                 [f"PBH{h}_{s}" for s in range(NSUB)], writes=["P6"])
            S.op("scalar", lambda e: e.activation(out=PBH[:, h, 0, :], in_=PST[:, h, 0, :], func=AF.Copy),
                 reads=[f"PST{h}_0"], writes=[f"PBH{h}_0"])
            S.op("vector", lambda e: [e.bn_stats(out=BNS[:, s, :], in_=PS[6][:, s * 128:(s + 1) * 128]) for s in range(NSUB)],
                 reads=["P6"], writes=["BNS"])
            S.op("vector", lambda e: [e.bn_aggr(out=MV[:, s, :], in_=BNS[:, s, :]) for s in range(NSUB)],
                 reads=["BNS"], writes=["MV"])
            norm_coefs(STAT[:, 16:20], MV[:, 0:4, 1], STAT[:, 20:24], MV[:, 0:4, 0], 4, ["MV"], "rsr", "nmr")
            S.op("scalar", lambda e: [e.activation(out=rn[:, s, :], in_=PS[6][:, s * 128:(s + 1) * 128], func=AF.Identity,
                                                   scale=STAT[:, 16 + s:17 + s], bias=STAT[:, 20 + s:21 + s])
                                      for s in range(NSUB)],
                 reads=["P6", "rsr", "nmr"], writes=[f"RN{par}"])

        def st_B(h):
            par = h % 2
            rn = RN[par]
            S.op("tensor", lambda e: [e.transpose(out=psb(7)[:, s * 128:(s + 1) * 128], in_=rn[:, s, :],
                                                  identity=IDENT[:, :]) for s in range(NSUB)],
                 reads=[f"RN{par}", "IDENT"], writes=["P7"])
            S.op("vector", lambda e: e.tensor_tensor(out=MIXT[:, h, :], in0=psb(7)[:, 0:512], in1=SGT[par][:, :], op=ALU.mult),
                 reads=["P7", f"SGT{par}"], writes=[f"MIXT{h}"])

        def proj_qk(h):
            par = h % 2
            bk = win_fm(512 + h * 128, 1)
            rope(bk, 1, par, False, h)
            bq = win_fm(h * 128, 0)
            rope(bq, 0, par, True, h)

        def proj_g(h):
            par = h % 2
            bg = win_fm(1536 + h * 128, 3)
            S.op("scalar", lambda e: e.activation(out=SG[par][:, :], in_=PS[bg][:, :], func=AF.Silu),
                 reads=[f"P{bg}"], writes=[f"SG{par}"])
            S.op("gpsimd", lambda e: e.tensor_scalar(out=SGT[par][:, :], in0=SG[par][:, :],
                                                     scalar1=COLS[:, GNC + h:GNC + h + 1], scalar2=1.0,
                                                     op0=ALU.mult, op1=ALU.mult),
                 reads=[f"SG{par}", "COLS"], writes=[f"SGT{par}"])

        def proj_vg(s):
            b = win_tm(2560, 5, s)
            gv = GV[s % 2]
            gn = f"GV{s % 2}"
            S.op("scalar", lambda e: e.activation(out=gv[:, :], in_=PS[b][:, :], func=AF.Gelu_apprx_tanh),
                 reads=[f"P{b}"], writes=[gn])
            S.op("vector", lambda e: e.bn_stats(out=BNS[:, 4 + s, :], in_=gv[:, :]), reads=[gn], writes=[f"BNSg{s}"])
            S.op("vector", lambda e: e.bn_aggr(out=MV[:, 4 + s, :], in_=BNS[:, 4 + s, :]), reads=[f"BNSg{s}"], writes=[f"MVg{s}"])
            norm_coefs(STAT[:, 24 + s:25 + s], MV[:, 4 + s, 1:2], STAT[:, 28 + s:29 + s], MV[:, 4 + s, 0:1], 1,
                       [f"MVg{s}"], f"rsg{s}", f"nmg{s}")
            S.op("gpsimd", lambda e: e.tensor_scalar(out=NN[:, s, :], in0=gv[:, :], scalar1=STAT[:, 24 + s:25 + s],
                                                     scalar2=STAT[:, 28 + s:29 + s], op0=ALU.mult, op1=ALU.add),
                 reads=[gn, f"rsg{s}", f"nmg{s}"], writes=[f"NN{s}"])

        def proj_u(g):
            par = g % 2
            bu = win_fm(2048 + g * 128, 4)
            S.op("scalar", lambda e: e.activation(out=GUT[par][:, :], in_=PS[bu][:, :], func=AF.Gelu_apprx_tanh),
                 reads=[f"P{bu}"], writes=[f"GUT{par}"])

        def st_MM(g):
            par = g % 2
            S.op("tensor", lambda e: [e.matmul(PS[7][:, s * 128:(s + 1) * 128], lhsT=NN[:, s, g * 128:(g + 1) * 128],
                                               rhs=WMT[:, g, :], start=True, stop=True) for s in range(NSUB)],
                 reads=[f"NN{s}" for s in range(NSUB)] + ["WMT"], writes=["P7"])
            S.op("vector", lambda e: e.scalar_tensor_tensor(
                out=T1[0][:, :].rearrange("p (s t) -> p s t", s=NSUB), in0=PS[7][:, :].rearrange("p (s t) -> p s t", s=NSUB),
                scalar=COLS[:, LGC + g:LGC + g + 1], in1=B2[:, g:g + 1, :].to_broadcast([128, NSUB, 128]),
                op0=ALU.mult, op1=ALU.add),
                reads=["P7", "COLS", "B2"], writes=["T1_0"])
            S.op("gpsimd", lambda e: e.tensor_tensor(out=MIXT[:, 4 + g, :], in0=T1[0][:, :], in1=GUT[par][:, :], op=ALU.mult),
                 reads=["T1_0", f"GUT{par}"], writes=[f"MIXT{4 + g}"])

        def wo_mm(s, wo_slots):
            b0, b1 = (0, 1) if s % 2 == 0 else (2, 3)

            order = [0, 1, 2, 4, 5, 6, 3, 7]

            def womm(e, cs):
                out = []
                for hf, bb in ((0, b0), (1, b1)):
                    for c in cs:
                        slot = wo_slots[c // 4]
                        out.append(e.matmul(PS[bb][:, :], lhsT=MIXT[:, c, s * 128:(s + 1) * 128],
                                            rhs=RING[slot][:, (c % 4) * 1024 + hf * 512:(c % 4) * 1024 + hf * 512 + 512],
                                            start=(c == order[0]), stop=(c == order[-1])))
                return out
            S.op("tensor", lambda e: womm(e, order[:5]), reads=[f"MIXT{c}" for c in order[:5]] + [f"RING{j}" for j in wo_slots],
                 writes=[f"P{b0}", f"P{b1}"])
            S.op("tensor", lambda e: womm(e, order[5:]), reads=[f"MIXT{c}" for c in order[5:]] + [f"RING{j}" for j in wo_slots],
                 writes=[f"P{b0}", f"P{b1}"])
            S.op("vector", lambda e: [
                e.tensor_tensor(out=XB[:, s, 0:512], in0=XB[:, s, 0:512], in1=PS[b0][:, :], op=ALU.add),
                e.tensor_tensor(out=XB[:, s, 512:1024], in0=XB[:, s, 512:1024], in1=PS[b1][:, :], op=ALU.add)],
                reads=[f"XB{s}", f"P{b0}", f"P{b1}"], writes=[f"XB{s}"])
            xs = XS[s % 2]
            xn = f"XS{s % 2}"
            S.op("scalar", lambda e: e.activation(out=xs[:, :], in_=XB[:, s, :], func=AF.Square,
                                                  accum_out=STAT[:, 8 + s:9 + s]),
                 reads=[f"XB{s}"], writes=[xn, f"ss2_{s}"])
            pow_rstd(STAT[:, 12 + s:13 + s], STAT[:, 8 + s:9 + s], 1, [f"ss2_{s}"], [f"rs2_{s}"], scale=1.0 / D)
            S.op("scalar", lambda e: e.activation(out=xs[:, :], in_=XB[:, s, :], func=AF.Copy,
                                                  scale=STAT[:, 12 + s:13 + s]),
                 reads=[f"XB{s}", f"rs2_{s}"], writes=[xn])

        def wo_tr(s):
            xs = XS[s % 2]
            xn = f"XS{s % 2}"
            S.op("tensor", lambda e: [e.transpose(out=psb(4)[:, k * 128:(k + 1) * 128], in_=xs[:, k * 128:(k + 1) * 128],
                                                  identity=IDENT[:, :]) for k in range(8)],
                 reads=[xn, "IDENT"], writes=["P4"])
            S.op("vector", lambda e: e.tensor_tensor(out=H2T[:, :, s * 128:(s + 1) * 128],
                                                     in0=psb(4).rearrange("p (k t) -> p k t", k=8),
                                                     in1=COLS[:, G2C:G2C + 8].unsqueeze(2).to_broadcast([128, 8, 128]),
                                                     op=ALU.mult),
                 reads=["P4", "COLS"], writes=[f"H2T{s}"])

        prepB(0, 0)
        prepA(0, 2, True)
        prepB(0, 1)
        prepA(0, 3, True)
        prepB(0, 2)
        prepB(0, 3)

        for i in range(NT):
            wo_slots = [ring_load(f"wo{b}", 4096) for b in range(2)]
            if i > 0:
                for s in range(NSUB):
                    S.op("sync", lambda e, s=s, i=i: e.dma_start(out=XB[:, s, :], in_=x_t[i, s]), writes=[f"XB{s}"], dma=f"xb_{s}")

            for s in range(NSUB):
                b = win_tm(1024, 2, s)
                S.op("scalar", lambda e, s=s, b=b: e.activation(out=V[:, s, :], in_=PS[b][:, :], func=AF.Copy),
                     reads=[f"P{b}"], writes=[f"V{s}"])
            proj_qk(0)
            proj_g(0)
            st_T(0)
            proj_vg(0)
            proj_vg(1)
            for h in range(4):
                st_S(h)
                st_KV(h)
                if h < 3:
                    proj_qk(h + 1)
                else:
                    proj_vg(2)
                    proj_vg(3)
                if h >= 1:
                    st_B(h - 1)
                if h < 3:
                    proj_g(h + 1)
                    st_T(h + 1)
                else:
                    proj_u(0)
                st_RET(h)
            proj_u(1)
            st_MM(0)
            proj_u(2)
            st_MM(1)
            proj_u(3)
            st_MM(2)
            st_B(3)
            st_MM(3)
            if i + 1 < NT:
                S.op("sync", lambda e, i=i: e.dma_start(out=CT[:, :], in_=ct_d[i + 1]), writes=["CT"], dma="t_c")
                S.op("sync", lambda e, i=i: e.dma_start(out=ST[:, :], in_=st_d[i + 1]), writes=["ST"], dma="t_s")

            wo_mm(0, wo_slots)
            for s in range(1, NSUB):
                wo_mm(s, wo_slots)
                wo_tr(s - 1)

            prep_next = 0
            for hf in range(2):
                gu_blocks = list(range(0, HALF, 2))
                dn_blocks = list(range(0, HALF, 4))
                gu_slot = {}
                pend = [("gu", b) for b in gu_blocks] + [("dn", b) for b in dn_blocks]
                loaded = {}
                nxt = 0

                def ensure(upto):
                    nonlocal nxt
                    while nxt < len(pend) and nxt <= upto:
                        kind, b = pend[nxt]
                        n = min(2 if kind == "gu" else 4, HALF - b)
                        ncols = n * (2048 if kind == "gu" else 1024)
                        loaded[(kind, b)] = ring_load(f"{kind}{hf}_{b}", ncols)
                        nxt += 1

                late = []

                def evac_gu(pb, cl):
                    sg = SG[cl % 2]
                    S.op("scalar", lambda e: e.activation(out=sg[:, :], in_=PS[pb][:, :], func=AF.Silu),
                         reads=[f"P{pb}"], writes=[f"SG{cl % 2}"])
                    S.op("vector", lambda e: e.tensor_tensor(out=HID[:, cl, :], in0=sg[:, :], in1=PS[pb + 1][:, :], op=ALU.mult),
                         reads=[f"SG{cl % 2}", f"P{pb + 1}"], writes=[f"HID{cl}"])

                ensure(NRING - 3)
                for cl in range(HALF):
                    blk = (cl // 2) * 2
                    ensure(gu_blocks.index(blk) + NRING - 2)
                    slot = loaded[("gu", blk)]
                    off = (cl - blk) * 2048
                    pb = 0 if cl % 2 == 0 else 2
                    def gumm(e, slot=slot, off=off, pb=pb, c0=0, c1=T):
                        return ([e.matmul(PS[pb][:, c0:c1], lhsT=RING[slot][:, off + k * 128:off + (k + 1) * 128],
                                          rhs=H2T[:, k, c0:c1], start=(k == 0), stop=(k == 7)) for k in range(8)] +
                                [e.matmul(PS[pb + 1][:, c0:c1], lhsT=RING[slot][:, off + 1024 + k * 128:off + 1024 + (k + 1) * 128],
                                          rhs=H2T[:, k, c0:c1], start=(k == 0), stop=(k == 7)) for k in range(8)])
                    if hf == 0 and cl < 2:
                        S.op("tensor", lambda e, g=gumm: g(e, c0=0, c1=384), reads=H2Tall[:3] + [f"RING{slot}"],
                             writes=[f"P{pb}", f"P{pb + 1}"])
                        late.append((gumm, slot, pb, cl))
                        if cl == 1:
                            wo_tr(NSUB - 1)
                            for (g2, sl2, pb2, cl2) in late:
                                S.op("tensor", lambda e, g=g2: g(e, c0=384, c1=512), reads=H2Tall[3:] + [f"RING{sl2}"],
                                     writes=[f"P{pb2}", f"P{pb2 + 1}"])
                                evac_gu(pb2, cl2)
                    else:
                        S.op("tensor", gumm, reads=H2Tall + [f"RING{slot}"], writes=[f"P{pb}", f"P{pb + 1}"])
                        evac_gu(pb, cl)
                    if i + 1 < NT and hf == 0:
                        if cl in (1, 3, 5, 7):
                            prepA(i + 1, (cl - 1) // 2)
                        if cl in (4, 6, 8, 10):
                            prepB(i + 1, (cl - 4) // 2)
                for cl in range(HALF):
                    blk = (cl // 4) * 4
                    ensure(len(gu_blocks) + dn_blocks.index(blk) + 1)
                    slot = loaded[("dn", blk)]
                    off = (cl - blk) * 1024
                    S.op("tensor", lambda e, slot=slot, off=off, cl=cl: [
                        e.matmul(PS[s * 2 + h2][:, :], lhsT=HID[:, cl, s * 128:(s + 1) * 128],
                                 rhs=RING[slot][:, off + h2 * 512:off + h2 * 512 + 512],
                                 start=(cl == 0), stop=(cl == HALF - 1))
                        for s in range(NSUB) for h2 in range(2)],
                        reads=([f"HID{c}" for c in range(HALF)] if cl == 0 else [f"HID{cl}"]) + [f"RING{slot}"],
                        writes=[f"P{j}" for j in range(8)])
                for s in range(NSUB):
                    S.op("vector", lambda e, s=s: [
                        e.tensor_tensor(out=XB[:, s, 0:512], in0=XB[:, s, 0:512], in1=PS[2 * s][:, :], op=ALU.add),
                        e.tensor_tensor(out=XB[:, s, 512:1024], in0=XB[:, s, 512:1024], in1=PS[2 * s + 1][:, :], op=ALU.add)],
                        reads=[f"XB{s}", f"P{2 * s}", f"P{2 * s + 1}"], writes=[f"XB{s}"])

            for s in range(NSUB):
                xs = XS[s % 2]
                xn = f"XS{s % 2}"
                S.op("scalar", lambda e, s=s, xs=xs: e.activation(out=xs[:, :], in_=XB[:, s, :], func=AF.Square,
                                                                  accum_out=STAT[:, 32 + s:33 + s]),
                     reads=[f"XB{s}"], writes=[xn, f"ss3_{s}"])
                pow_rstd(STAT[:, 36 + s:37 + s], STAT[:, 32 + s:33 + s], 1, [f"ss3_{s}"], [f"rs3_{s}"], scale=1.0 / D)
                S.op("vector", lambda e, s=s: e.scalar_tensor_tensor(out=XB[:, s, :], in0=XB[:, s, :], scalar=STAT[:, 36 + s:37 + s],
                                                                     in1=FGB[:, :], op0=ALU.mult, op1=ALU.mult),
                     reads=[f"XB{s}", f"rs3_{s}", "FGB"], writes=[f"XB{s}"])
                S.op("sync", lambda e, s=s, i=i: e.dma_start(out=y_t[i, s], in_=XB[:, s, :]), reads=[f"XB{s}"], dma=f"y_{s}")

        S.final_wait("sync", [f"y_{s}" for s in range(NSUB)])
        S.emit()
    return nc


def _prep_shared(norm1_g, w_in, ret_gn_g, gmlp_ln_g, gmlp_ln_b, w_s, b_s, w_out, norm2_g,
                 w_ffn_gate, w_ffn_up, w_ffn_down, final_g):
    c = _consts()
    f = np.float32
    w_in_l = np.ascontiguousarray(np.asarray(w_in[0], f).reshape(8, 128, 3072).transpose(1, 0, 2))
    w_out_l = np.ascontiguousarray(np.asarray(w_out[0], f).reshape(8, 128, 1024).transpose(1, 0, 2))
    wg = np.asarray(w_ffn_gate[0], f).reshape(8, 128, NFC, 128)
    wu = np.asarray(w_ffn_up[0], f).reshape(8, 128, NFC, 128)
    gu = np.stack([wg, wu], axis=0)
    w_gu_l = np.ascontiguousarray(gu.transpose(2, 3, 0, 1, 4).reshape(128, NFC, 2048))
    w_dn_l = np.ascontiguousarray(np.asarray(w_ffn_down[0], f).reshape(NFC, 128, 1024).transpose(1, 0, 2))
    cols = np.zeros((128, 32), f)
    cols[:, 0:8] = np.asarray(norm1_g[0], f).reshape(8, 128).T
    cols[:, 8:16] = np.asarray(norm2_g[0], f).reshape(8, 128).T
    cols[:, 16:20] = np.asarray(ret_gn_g[0], f).reshape(4, 128).T
    cols[:, 20:24] = np.asarray(gmlp_ln_g[0], f).reshape(4, 128).T
    cols[:, 24:28] = np.asarray(gmlp_ln_b[0], f).reshape(4, 128).T
    cols[:, 28:32] = c["kdec"]
    bbc = np.ascontiguousarray(np.broadcast_to(np.asarray(b_s[0], f)[None, :, :], (128, 4, 128)))
    wsT = np.ascontiguousarray(np.asarray(w_s[0], f).transpose(2, 0, 1))
    fgb = np.ascontiguousarray(np.broadcast_to(np.asarray(final_g, f)[None, :], (128, 1024)))
    return {
        "w_in_l": w_in_l, "w_out_l": w_out_l, "w_gu_l": w_gu_l, "w_dn_l": w_dn_l,
        "ct": c["ct"], "st": c["st"], "maskT": c["maskT"], "qdec": c["qdec"],
        "bbc": bbc, "wsT": wsT, "fgb": fgb, "cols": cols, "ident": np.eye(128, dtype=f),
    }


def kernel(x, norm1_g, w_in, ret_gn_g, gmlp_ln_g, gmlp_ln_b, w_s, b_s, w_out,
           norm2_g, w_ffn_gate, w_ffn_up, w_ffn_down, final_g):
    x = np.asarray(x, np.float32)
    shared = _prep_shared(norm1_g, w_in, ret_gn_g, gmlp_ln_g, gmlp_ln_b, w_s, b_s, w_out, norm2_g,
                          w_ffn_gate, w_ffn_up, w_ffn_down, final_g)
    nc = build_program(NT_FULL)
    in_maps = []
    for b in range(8):
        m = dict(shared)
        m["x"] = np.ascontiguousarray(x[b])
        in_maps.append(m)
    res = run_bass_kernel_spmd(nc, in_maps, core_ids=list(range(8)))
    return np.stack([np.asarray(r["y"], np.float32) for r in res.results], axis=0)
```
